# Optimizing a Trainium2 kernel written in Bass

```python
import jax, jax.numpy as jnp
from jax import lax
import numpy as np

D_MODEL = 4096
BATCH = 4
SEQ = 2048
DEPTH = 2

HEAD_DIM = 128
N_HEADS = D_MODEL // 256
N_KV_HEADS = N_HEADS // 4
GROUP = N_HEADS // N_KV_HEADS
ATTN_WIDTH = N_HEADS * HEAD_DIM
KV_WIDTH = N_KV_HEADS * HEAD_DIM
WINDOW = 128
BLOCK = 128
ROPE_THETA = 500000.0
ROT_DIM = HEAD_DIM // 4
POOL_WIDTH = D_MODEL // 2
POOL_WINDOWS = (2, 4, 8, 16)
N_POOL_GROUPS = len(POOL_WINDOWS)
POOL_GROUP_DIM = POOL_WIDTH // N_POOL_GROUPS
IN_SPLITS = (ATTN_WIDTH, ATTN_WIDTH + KV_WIDTH, ATTN_WIDTH + 2 * KV_WIDTH,
             ATTN_WIDTH + 2 * KV_WIDTH + POOL_WIDTH,
             ATTN_WIDTH + 2 * KV_WIDTH + POOL_WIDTH + D_MODEL)
IN_WIDTH = ATTN_WIDTH + 2 * KV_WIDTH + POOL_WIDTH + 2 * D_MODEL
D_FF = ((-((-8 * D_MODEL) // 3)) + 255) // 256 * 256
RMS_EPS = 1e-6

kernel_name = 'hybrid_window_gqa_multiscale_pool_gated_encoder'


def rms_norm(x, g):
    x32 = x.astype(jnp.float32)
    y = x32 * lax.rsqrt(jnp.mean(x32 * x32, axis=-1, keepdims=True) + RMS_EPS)
    return (y * g.astype(jnp.float32)).astype(x.dtype)


def rotary_tables(seq):
    pos = jnp.arange(seq, dtype=jnp.float32)
    inv_freq = 1.0 / jnp.power(jnp.float32(ROPE_THETA),
                               jnp.arange(0, ROT_DIM, 2, dtype=jnp.float32) / ROT_DIM)
    ang = pos[:, None] * inv_freq[None, :]
    return jnp.cos(ang), jnp.sin(ang)


def partial_rotary(t, cos, sin):
    half = ROT_DIM // 2
    c = cos[None, :, None, :].astype(t.dtype)
    s = sin[None, :, None, :].astype(t.dtype)
    t1 = t[..., :half]
    t2 = t[..., half:ROT_DIM]
    return jnp.concatenate([t1 * c - t2 * s, t2 * c + t1 * s, t[..., ROT_DIM:]], axis=-1)


def windowed_gqa_with_sink(q, k, v, sink):
    B, S = q.shape[0], q.shape[1]
    nb = S // BLOCK
    qb = q.reshape(B, nb, BLOCK, N_KV_HEADS, GROUP, HEAD_DIM)
    pad = ((0, 0), (BLOCK, BLOCK), (0, 0), (0, 0))
    kp = jnp.pad(k, pad).reshape(B, nb + 2, BLOCK, N_KV_HEADS, HEAD_DIM)
    vp = jnp.pad(v, pad).reshape(B, nb + 2, BLOCK, N_KV_HEADS, HEAD_DIM)
    kw = jnp.concatenate([kp[:, :-2], kp[:, 1:-1], kp[:, 2:]], axis=2)
    vw = jnp.concatenate([vp[:, :-2], vp[:, 1:-1], vp[:, 2:]], axis=2)
    s = jnp.einsum('bnqhgd,bnjhd->bnhgqj', qb, kw).astype(jnp.float32) * (HEAD_DIM ** -0.5)
    blk = jnp.arange(nb)[:, None] * BLOCK
    qpos = blk + jnp.arange(BLOCK)[None, :]
    kpos = blk - BLOCK + jnp.arange(3 * BLOCK)[None, :]
    valid = ((kpos[:, None, :] >= 0) & (kpos[:, None, :] < S)
             & (jnp.abs(qpos[:, :, None] - kpos[:, None, :]) <= WINDOW))
    s = jnp.where(valid[None, :, None, None], s, -jnp.inf)
    sk = sink.astype(jnp.float32).reshape(1, 1, N_KV_HEADS, GROUP, 1, 1)
    m = jnp.maximum(jnp.max(s, axis=-1, keepdims=True), sk)
    p = jnp.exp(s - m)
    denom = jnp.sum(p, axis=-1, keepdims=True) + jnp.exp(sk - m)
    p = (p / denom).astype(v.dtype)
    o = jnp.einsum('bnhgqj,bnjhd->bnqhgd', p, vw)
    return o.reshape(B, S, ATTN_WIDTH)


def multiscale_pool(u, pool_w, pool_scale):
    B, S = u.shape[0], u.shape[1]
    ug = u.astype(jnp.float32).reshape(B, S, N_POOL_GROUPS, POOL_GROUP_DIM)
    t = jnp.arange(S)
    outs = []
    for gi, w in enumerate(POOL_WINDOWS):
        x_g = ug[:, :, gi]
        csum = jnp.concatenate([jnp.zeros((B, 1, POOL_GROUP_DIM), jnp.float32),
                                lax.cumsum(x_g, axis=1)], axis=1)
        lo = jnp.clip(t - w // 2, 0, S - 1)
        hi = jnp.clip(t + w // 2 - 1, 0, S - 1)
        cnt = (hi - lo + 1).astype(jnp.float32)[None, :, None]
        mean = (csum[:, hi + 1] - csum[:, lo]) / cnt
        outs.append(mean - x_g)
    pooled = jnp.stack(outs, axis=2)
    mixed = jnp.einsum('bsgc,gcd->bsgd', pooled, pool_w.astype(jnp.float32))
    mixed = mixed.reshape(B, S, POOL_WIDTH) * pool_scale.astype(jnp.float32)
    return mixed.astype(u.dtype)


def setup_inputs(seed: int = 0) -> dict:
    key = jax.random.key(seed)
    ks = jax.random.split(key, 14)
    f32 = jnp.float32

    def w(k, shape, fan_in):
        return jax.random.normal(k, shape, f32) * (fan_in ** -0.5)

    def gain(k, shape):
        return 1.0 + 0.02 * jax.random.normal(k, shape, f32)

    return {
        'x': jax.random.normal(ks[0], (BATCH, SEQ, D_MODEL), f32),
        'norm1_g': gain(ks[1], (DEPTH, D_MODEL)),
        'w_in': w(ks[2], (DEPTH, D_MODEL, IN_WIDTH), D_MODEL),
        'attn_sink': 0.5 * jax.random.normal(ks[3], (DEPTH, N_HEADS), f32),
        'pool_w': w(ks[4], (DEPTH, N_POOL_GROUPS, POOL_GROUP_DIM, POOL_GROUP_DIM), POOL_GROUP_DIM),
        'pool_scale': gain(ks[5], (DEPTH, POOL_WIDTH)),
        'w_branch_attn': w(ks[6], (DEPTH, ATTN_WIDTH, D_MODEL), ATTN_WIDTH),
        'w_branch_pool': w(ks[7], (DEPTH, POOL_WIDTH, D_MODEL), POOL_WIDTH),
        'w_out': w(ks[8], (DEPTH, D_MODEL, D_MODEL), D_MODEL),
        'norm2_g': gain(ks[9], (DEPTH, D_MODEL)),
        'w_gate_up': w(ks[10], (DEPTH, D_MODEL, 2 * D_FF), D_MODEL),
        'w_down': w(ks[11], (DEPTH, D_FF, D_MODEL), D_FF),
        'final_norm_g': gain(ks[12], (D_MODEL,)),
    }


def reference(x, norm1_g, w_in, attn_sink, pool_w, pool_scale, w_branch_attn,
              w_branch_pool, w_out, norm2_g, w_gate_up, w_down, final_norm_g):
    B, S = x.shape[0], x.shape[1]
    cos, sin = rotary_tables(S)
    for l in range(DEPTH):
        h = rms_norm(x, norm1_g[l])
        proj = h @ w_in[l]
        q, k, v, u, ga, gb = jnp.split(proj, IN_SPLITS, axis=-1)
        q = partial_rotary(q.reshape(B, S, N_HEADS, HEAD_DIM), cos, sin)
        k = partial_rotary(k.reshape(B, S, N_KV_HEADS, HEAD_DIM), cos, sin)
        v = v.reshape(B, S, N_KV_HEADS, HEAD_DIM)
        y_attn = windowed_gqa_with_sink(q, k, v, attn_sink[l]) @ w_branch_attn[l]
        y_pool = multiscale_pool(u, pool_w[l], pool_scale[l]) @ w_branch_pool[l]
        merged = jax.nn.sigmoid(ga) * y_attn + jax.nn.sigmoid(gb) * y_pool
        x = x + merged @ w_out[l]
        h2 = rms_norm(x, norm2_g[l])
        gate, up = jnp.split(h2 @ w_gate_up[l], 2, axis=-1)
        x = x + (jax.nn.silu(gate) * up) @ w_down[l]
    return rms_norm(x, final_norm_g)
```

```python
import numpy as np
import ml_dtypes
from contextlib import ExitStack
import concourse.bass as bass
import concourse.mybir as mybir
from concourse.bass_utils import run_bass_kernel_spmd

F32 = mybir.dt.float32
BF16 = mybir.dt.bfloat16
AF = mybir.ActivationFunctionType
ALU = mybir.AluOpType
AX = mybir.AxisListType

D = 4096
NCH = 32
SEQ = 2048
OWN = 1024
DEPTH = 2
DFF = 11008
NF = 86
FGROUPS = [(0, 16), (16, 32), (32, 48), (48, 64), (64, 80), (80, 86)]
TQ = [1152, 1024]
TK = [1280, 1152]
TX = 1280
EPS = 1e-6
NEG = -30000.0
SCALE = 128 ** -0.5
ENGS = ("pe", "dve", "act", "pool", "sp")
ND = 8
NCORES = 8
NVS = 1
WGROUPS = [("in", 108, 4096), ("pw", 16, 512), ("br", 64, 2048), ("out", 32, 4096),
           ("gu", 172, 4096), ("dn", 160, 2048), ("dl", 32, 768)]


def tiles_of(T):
    if T == 1280:
        return [(0, 512), (512, 512), (1024, 256)]
    if T == 1152:
        return [(0, 384), (384, 384), (768, 384)]
    if T == 1024:
        return [(0, 512), (512, 512)]
    raise ValueError(T)


class Sem:
    def __init__(self, h, idx):
        self.h = h
        self.n = 0
        self.idx = idx


class Rec:
    def __init__(self, esem, dsem):
        self.q = {e: [] for e in ENGS}
        self.seen = {e: {} for e in ENGS}
        self.lastw = {}
        self.readers = {}
        self.esem = esem
        self.dsem = dsem
        self.di = {q: 0 for q in dsem}

    def _waits(self, eng, tickets):
        out = []
        for t in tickets:
            if t is None:
                continue
            sem, val, src = t
            if src == eng and eng not in ("dve", "act"):
                continue
            cur = self.seen[eng].get(sem.idx, 0)
            if val > cur:
                self.seen[eng][sem.idx] = val
                out.append((sem.h, val))
        return out

    def _deps(self, reads, writes, extra):
        tickets = list(extra)
        for b in reads:
            tickets.append(self.lastw.get(b))
        for b in writes:
            tickets.append(self.lastw.get(b))
            tickets.extend(self.readers.get(b, {}).values())
        return tickets

    def _commit(self, tk, reads, writes):
        for b in reads:
            d = self.readers.setdefault(b, {})
            k = tk[0].idx
            if k not in d or d[k][1] < tk[1]:
                d[k] = tk
        for b in writes:
            self.lastw[b] = tk
            self.readers[b] = {}

    def op(self, eng, fn, reads=(), writes=(), extra=()):
        return self.multi(eng, [fn], reads, writes, extra)

    def multi(self, eng, fns, reads=(), writes=(), extra=()):
        waits = self._waits(eng, self._deps(reads, writes, extra))
        sem = self.esem[eng]
        sem.n += 1
        tk = (sem, sem.n, eng)
        n = len(fns)
        for i, fn in enumerate(fns):
            self.q[eng].append((fn, waits if i == 0 else [], (sem.h, 1) if i == n - 1 else None))
        self._commit(tk, reads, writes)
        return tk

    def dma(self, q, out, in_, reads=(), writes=(), extra=()):
        sems = self.dsem[q]
        sem = sems[self.di[q] % len(sems)]
        self.di[q] += 1
        prev = (sem, sem.n, "dma") if sem.n else None
        waits = self._waits(q, self._deps(reads, writes, list(extra) + [prev]))
        sem.n += 16
        tk = (sem, sem.n, "dma")
        self.q[q].append((lambda e: e.dma_start(out=out, in_=in_), waits, (sem.h, 16)))
        self._commit(tk, reads, writes)
        return tk

    def custom(self, eng, fn, sem, amt, tickets, src):
        waits = self._waits(eng, tickets)
        sem.n += amt
        tk = (sem, sem.n, src)
        self.q[eng].append((fn, waits, (sem.h, amt)))
        return tk

    def wait_only(self, eng, tickets):
        waits = self._waits(eng, tickets)
        if waits:
            self.q[eng].append((None, waits, None))

    def barrier(self, engs=("pe", "dve", "act", "sp")):
        for e in engs:
            tickets = [(s, s.n, x) for x, s in self.esem.items() if x in engs and x != e and s.n]
            tickets += [(s, s.n, "dma") for s in self.dsem["sp"] if s.n]
            self.wait_only(e, tickets)

    def replay(self, name, eng):
        for fn, waits, sig in self.q[name]:
            for h, v in waits:
                eng.wait_ge(h, v)
            if fn is None:
                continue
            inst = fn(eng)
            if sig is not None:
                inst.then_inc(sig[0], sig[1])


def f_mm(out, lhsT, rhs, start, stop):
    return lambda e: e.matmul(out, lhsT=lhsT, rhs=rhs, start=start, stop=stop)


def f_tr(out, in_, ident):
    return lambda e: e.transpose(out=out, in_=in_, identity=ident)


def f_act(out, in_, func, **kw):
    return lambda e: e.activation(out=out, in_=in_, func=func, **kw)


def f_tt(out, in0, in1, op):
    return lambda e: e.tensor_tensor(out=out, in0=in0, in1=in1, op=op)


def f_ts(out, in0, s1, s2, op0, op1=None):
    if op1 is None:
        return lambda e: e.tensor_scalar(out=out, in0=in0, scalar1=s1, scalar2=None, op0=op0)
    return lambda e: e.tensor_scalar(out=out, in0=in0, scalar1=s1, scalar2=s2, op0=op0, op1=op1)


def f_stt(out, in0, scalar, in1, op0, op1):
    return lambda e: e.scalar_tensor_tensor(out=out, in0=in0, scalar=scalar, in1=in1, op0=op0, op1=op1)


def f_copy(out, in_):
    return lambda e: e.tensor_copy(out=out, in_=in_)


def f_memset(ap, v):
    return lambda e: e.memset(ap, v)


def f_rmax(out, in_):
    return lambda e: e.reduce_max(out=out, in_=in_, axis=AX.X)


def f_recip(out, in_):
    return lambda e: e.reciprocal(out=out, in_=in_)


class _Stop(Exception):
    pass


def build(NL=DEPTH, debug=False, stop=99, nvs=NVS, groups=None, alev=9):
    nc = bass.Bass("TRN2", target_bir_lowering=False)

    def din(name, shape, dt=F32):
        return nc.dram_tensor(name, shape, dt, kind="ExternalInput")

    def dscr(name, shape, dt):
        if debug:
            return nc.dram_tensor(name, shape, dt, kind="ExternalOutput")
        return nc.dram_tensor(name, shape, dt)

    xT = din("xT", [NVS * D, TX])
    yT = nc.dram_tensor("yT", [NVS * D, OWN], F32, kind="ExternalOutput")
    gvec = din("gvec", [128, 5 * NCH])
    pscale_d = din("pscale", [128, DEPTH * 16])
    sinkb_d = din("sinkb", [128, DEPTH * 16])
    ctab_d = din("ctab", [2 * 128, TX])
    stab_d = din("stab", [2 * 128, TX])
    mask_d = din("masks", [128, 640])
    ident_d = din("ident", [128, 128], BF16)
    poolc_d = din("poolc", [2 * 128, 66])
    wts = {}
    for l in range(NL):
        for (g, npan, W) in WGROUPS:
            if groups is not None and g not in groups:
                continue
            wts[l, g] = din(f"w{l}_{g}", [npan * 128, W])
    xres = dscr("xres", [D, 1152], F32)
    attn_h = dscr("attn_h", [2048, 1152], BF16)
    mixed_h = dscr("mixed_h", [2048, 1152], BF16)
    sg_h = dscr("sg_h", [2 * D, 1152], BF16)

    off = [16640]

    def at(name, shape, dt, offset):
        return nc.alloc_sbuf_tensor_at(name, shape, dt, offset=offset)

    def bump(nbytes):
        o = off[0]
        off[0] += (nbytes + 31) // 32 * 32
        return o

    X0 = bump(NCH * 1280 * 2)
    Y0 = bump(NCH * 1152 * 2)
    W0 = bump(3 * 8192)
    S0 = bump(4 * 5120)
    T0 = bump(4 * 2048)
    XA = at("XA", [128, NCH, 1280], BF16, X0)
    YH = at("YH", [128, NCH, 1152], BF16, Y0)
    WR = at("WR", [128, 3, 4096], BF16, W0)
    STG = at("STG", [128, 4, 1280], F32, S0)
    GST = at("GST", [128, 2, 2, 1152], BF16, S0)
    TMP = at("TMP", [128, 4, 512], F32, T0)
    rstdY = at("rstdY", [128, 1280], F32, Y0 + 62464)
    masks = at("masks_sb", [128, 640], F32, Y0 + 62464 + 5120)
    rstdX = at("rstdX", [128, 1280], F32, X0 + 40960)
    ident = at("ident_sb", [128, 128], BF16, bump(256))
    ones = at("ones_sb", [128, 128], F32, bump(512))
    gv = at("gv_sb", [128, 5 * NCH], F32, bump(640))
    psc = at("psc_sb", [128, 32], F32, bump(128))
    snk = at("snk_sb", [128, 32], F32, bump(128))
    nsnk = at("nsnk_sb", [128, 32], F32, bump(128))
    poolc = at("poolc_sb", [128, 66], F32, bump(264))
    cols = at("cols_sb", [128, 16], F32, bump(64))
    assert off[0] <= 224 * 1024, off[0]
    yo = [Y0]

    def ybump(n):
        o = yo[0]
        yo[0] += (n + 31) // 32 * 32
        assert yo[0] <= Y0 + 62464
        return o

    kT = at("kT", [128, 8, 1280], BF16, ybump(20480))
    Vtm = at("Vtm", [128, 10, 512], BF16, ybump(10240))
    qxy = at("qxy", [128, 4, 1152], BF16, ybump(9216))
    vfm = at("vfm", [128, 1280], BF16, ybump(2560))
    crs = at("crs", [128, 1280], F32, ybump(5120))
    srs = at("srs", [128, 1280], F32, ybump(5120))
    ssb = at("ssb", [128, 384], F32, ybump(1536))
    psb = at("psb", [128, 384], F32, ybump(1536))
    pn = at("pn", [128, 384], BF16, ybump(768))
    pT = at("pT", [128, 384], BF16, ybump(768))
    ost = at("ost", [128, 2, 1152], BF16, ybump(4608))
    yo[0] = Y0
    ubuf = at("ubuf", [128, 1296], F32, ybump(5184))
    pA = at("pA", [128, 1296], F32, ybump(5184))
    pB = at("pB", [128, 1296], F32, ybump(5184))
    pr = at("pr", [128, 1296], F32, ybump(5184))
    pooled = at("pooled", [128, 4, 1152], BF16, ybump(9216))
    mst = at("mst", [128, 2, 1152], BF16, ybump(4608))
    sgst = at("sgst", [128, 2, 1152], BF16, ybump(4608))
    accY = at("accY", [128, 1280], F32, Y0)
    sqY = at("sqY", [128, 1280], F32, Y0 + 5120)
    accX = at("accX", [128, 1280], F32, X0)
    sqX = at("sqX", [128, 1280], F32, X0 + 5120)
    actb = at("actb", [128, 16, 1152], BF16, X0)

    with ExitStack() as es:
        pb = [es.enter_context(nc.psum_tensor(f"pb{b}", [128, 512], F32)) for b in range(7)]
        ptb = es.enter_context(nc.psum_tensor("ptb", [128, 1024], BF16))
        nsem = [0]

        def mksem(name):
            s = Sem(es.enter_context(nc.semaphore(name)), nsem[0])
            nsem[0] += 1
            return s

        esem = {e: mksem(f"s_{e}") for e in ENGS}
        dsem = {q: [mksem(f"d_{q}{i}") for i in range(ND)] for q in ("sp", "pool")}
        cc_sem = mksem("cc")
        R = Rec(esem, dsem)

        R.dma("sp", gv[:], gvec[:, :], writes=["gv"])
        R.dma("sp", psc[:], pscale_d[:, :], writes=["psc"])
        R.dma("sp", snk[:], sinkb_d[:, :], writes=["snk"])
        R.dma("sp", ident[:], ident_d[:, :], writes=["ident"])
        R.op("dve", f_memset(ones[:], 1.0), writes=["ones"])
        R.op("dve", f_ts(nsnk[:], snk[:], -1.0, None, ALU.mult), reads=["snk"], writes=["nsnk"])

        wi = [0]

        def wload(l, g, idx, W):
            slot = wi[0] % 3
            wi[0] += 1
            R.dma("pool", WR[:, slot, 0:W], wts[l, g][idx * 128:(idx + 1) * 128, :], writes=[("w", slot)])
            return slot

        us = [0]

        def unit(slot, KC, rhs, T, rhs_keys, s=None):
            if s is None:
                s = us[0]
                us[0] ^= 1
            tl = tiles_of(T)
            fns = []
            for kc in range(KC):
                for j, (t0, tn) in enumerate(tl):
                    fns.append(f_mm(pb[3 * s + j][:, 0:tn], WR[:, slot, kc * 128:(kc + 1) * 128],
                                    rhs(kc, t0, tn), kc == 0, kc == KC - 1))
            R.multi("pe", fns, reads=[("w", slot)] + list(rhs_keys),
                    writes=[("ps", 3 * s + j) for j in range(len(tl))])
            return s

        def compute_rstd(acc, T, rstd):
            for j, (t0, tn) in enumerate(tiles_of(T)):
                R.op("pe", f_mm(pb[j][:, 0:tn], ones[:], acc[:, t0:t0 + tn], True, True),
                     reads=["acc", "ones"], writes=[("ps", j)])
                R.op("act", f_act(rstd[:, t0:t0 + tn], pb[j][:, 0:tn], AF.Sqrt, scale=1.0 / D, bias=EPS),
                     reads=[("ps", j)], writes=["rstd"])
                R.op("dve", f_recip(rstd[:, t0:t0 + tn], rstd[:, t0:t0 + tn]), reads=["rstd"], writes=["rstd"])

        def prep(src, T, dst, gcol, acc, sq, copy_to_xres, rstd):
            for o in range(NCH):
                slot = o % 4
                stg = STG[:, slot, 0:T]
                R.dma("sp", stg, src[o * 128:(o + 1) * 128, 0:T], writes=[("stg", slot)])
                if dst is not None:
                    R.op("act", f_act(dst[:, o, 0:T], stg, AF.Copy, scale=gv[:, gcol + o:gcol + o + 1]),
                         reads=[("stg", slot), "gv"], writes=[("xh", o)])
                if o == 0:
                    R.op("act", f_act(acc[:, 0:T], stg, AF.Square), reads=[("stg", slot)], writes=["acc"])
                else:
                    R.op("act", f_act(sq[:, 0:T], stg, AF.Square), reads=[("stg", slot)], writes=["sq"])
                    R.op("dve", f_tt(acc[:, 0:T], acc[:, 0:T], sq[:, 0:T], ALU.add), reads=["sq", "acc"], writes=["acc"])
                if copy_to_xres:
                    R.dma("sp", xres[o * 128:(o + 1) * 128, 0:1152], STG[:, slot, 0:1152],
                          reads=[("stg", slot)], writes=[("xres", o)])
            compute_rstd(acc, T, rstd)

        XH_KEYS = [("xh", o) for o in range(NCH)]

        def ck(k):
            if stop <= k:
                raise _Stop()

        def layers(vs):
          sdir = vs % 2
          R.dma("sp", poolc[:], poolc_d[sdir * 128:(sdir + 1) * 128, :], writes=["poolc"])
          for l in range(NL):
            Tq, Tk = TQ[l], TK[l]
            nqb = Tq // 128
            if stop <= 0:
                for (g, npan, W) in WGROUPS:
                    if (l, g) not in wts:
                        continue
                    sl = wload(l, g, npan - 1, W)
                    R.op("dve", f_copy(TMP[:, 0, 0:128], WR[:, sl, 0:128]), reads=[("w", sl)], writes=[("tmp", 0)])
                raise _Stop()
            if l == 0:
                prep(xT[vs * D:(vs + 1) * D, :], Tk, XA, (2 * l) * NCH, accY, sqY, True, rstdY)
            else:
                prep(xres, Tk, XA, (2 * l) * NCH, accY, sqY, False, rstdY)
            R.barrier()
            ck(1)
            rstd = rstdY
            R.dma("sp", masks[:], mask_d[:, :], writes=["masks"])
            R.dma("sp", crs[:, 0:Tk], ctab_d[sdir * 128:(sdir + 1) * 128, 0:Tk], writes=["crs"])
            R.dma("sp", srs[:, 0:Tk], stab_d[sdir * 128:(sdir + 1) * 128, 0:Tk], writes=["srs"])
            R.op("dve", f_tt(crs[:, 0:Tk], crs[:, 0:Tk], rstd[:, 0:Tk], ALU.mult), reads=["crs", "rstd"], writes=["crs"])
            R.op("dve", f_tt(srs[:, 0:Tk], srs[:, 0:Tk], rstd[:, 0:Tk], ALU.mult), reads=["srs", "rstd"], writes=["srs"])

            def rhs_x(kc, t0, tn):
                return XA[:, kc, t0:t0 + tn]

            def rot_pair(idxX, idxY, T, dstX, dstY, kx, ky):
                sx = wload(l, "in", idxX, 4096)
                unit(sx, NCH, rhs_x, T, XH_KEYS, s=0)
                sy = wload(l, "in", idxY, 4096)
                unit(sy, NCH, rhs_x, T, XH_KEYS, s=1)
                for j, (t0, tn) in enumerate(tiles_of(T)):
                    Xp, Yp = pb[j][:, 0:tn], pb[3 + j][:, 0:tn]
                    c_, s_ = crs[:, t0:t0 + tn], srs[:, t0:t0 + tn]
                    R.op("dve", f_tt(TMP[:, 0, 0:tn], Xp, c_, ALU.mult), reads=[("ps", j), "crs"], writes=[("tmp", 0)])
                    R.op("dve", f_tt(TMP[:, 1, 0:tn], Yp, s_, ALU.mult), reads=[("ps", 3 + j), "srs"], writes=[("tmp", 1)])
                    R.op("dve", f_tt(dstX[:, t0:t0 + tn], TMP[:, 0, 0:tn], TMP[:, 1, 0:tn], ALU.subtract),
                         reads=[("tmp", 0), ("tmp", 1)], writes=[kx])
                    R.op("dve", f_tt(TMP[:, 2, 0:tn], Yp, c_, ALU.mult), reads=[("ps", 3 + j), "crs"], writes=[("tmp", 2)])
                    R.op("dve", f_tt(TMP[:, 3, 0:tn], Xp, s_, ALU.mult), reads=[("ps", j), "srs"], writes=[("tmp", 3)])
                    R.op("dve", f_tt(dstY[:, t0:t0 + tn], TMP[:, 2, 0:tn], TMP[:, 3, 0:tn], ALU.add),
                         reads=[("tmp", 2), ("tmp", 3)], writes=[ky])

            for hk in range(4):
                rot_pair(2 * hk, 2 * hk + 1, Tk, kT[:, 2 * hk, :], kT[:, 2 * hk + 1, :], ("kT", 2 * hk), ("kT", 2 * hk + 1))
            nkb = Tk // 128
            for hk in range(4):
                sv = wload(l, "in", 8 + hk, 4096)
                s = unit(sv, NCH, rhs_x, Tk, XH_KEYS)
                for j, (t0, tn) in enumerate(tiles_of(Tk)):
                    R.op("dve", f_tt(vfm[:, t0:t0 + tn], pb[3 * s + j][:, 0:tn], rstd[:, t0:t0 + tn], ALU.mult),
                         reads=[("ps", 3 * s + j), "rstd"], writes=["vfm"])
                for b0 in range(0, nkb, 8):
                    nb = min(8, nkb - b0)
                    fns = [f_tr(ptb[:, i * 128:(i + 1) * 128], vfm[:, (b0 + i) * 128:(b0 + i + 1) * 128], ident[:])
                           for i in range(nb)]
                    R.multi("pe", fns, reads=["vfm", "ident"], writes=["ptb"])
                    R.op("act", f_copy_act(Vtm[:, b0:b0 + nb, hk * 128:(hk + 1) * 128],
                                           ptb[:, 0:nb * 128].rearrange("p (a b) -> p a b", b=128)),
                         reads=["ptb"], writes=["Vtm"])
            ck(2)
            for p in range(8):
                par = p % 2
                qx, qy = qxy[:, 2 * par, :], qxy[:, 2 * par + 1, :]
                rot_pair(12 + 2 * p, 13 + 2 * p, Tq, qx, qy, ("q", 2 * par), ("q", 2 * par + 1))
                hk = p // 2
                for hs in range(2):
                    h = 2 * p + hs
                    r = 64 * hs
                    sc = l * 16 + h
                    for qb in range(nqb if alev > 0 else 0):
                        if qb == 0:
                            k0, nk, mk = 0, 256, masks[:, 0:256]
                        else:
                            k0, nk, mk = (qb - 1) * 128, 384, masks[:, 256:640]
                        q0 = qb * 128
                        R.multi("pe", [
                            f_mm(pb[6][:, 0:nk], qx[r:r + 64, q0:q0 + 128], kT[r:r + 64, 2 * hk, k0:k0 + nk], True, False),
                            f_mm(pb[6][:, 0:nk], qy[r:r + 64, q0:q0 + 128], kT[r:r + 64, 2 * hk + 1, k0:k0 + nk], False, True)],
                            reads=[("q", 2 * par), ("q", 2 * par + 1), ("kT", 2 * hk), ("kT", 2 * hk + 1)], writes=["b6"])
                        R.op("dve", f_stt(ssb[:, 0:nk], pb[6][:, 0:nk], SCALE, mk, ALU.mult, ALU.add),
                             reads=["b6", "masks"], writes=["ssb"])
                        if alev < 2:
                            continue
                        R.op("dve", f_rmax(cols[:, 0:1], ssb[:, 0:nk]), reads=["ssb"], writes=["c0"])
                        R.op("dve", f_ts(cols[:, 1:2], cols[:, 0:1], -1.0, nsnk[:, sc:sc + 1], ALU.mult, ALU.min),
                             reads=["c0", "nsnk"], writes=["c1"])
                        R.op("act", f_act(psb[:, 0:nk], ssb[:, 0:nk], AF.Exp, bias=cols[:, 1:2], scale=1.0),
                             reads=["ssb", "c1"], writes=["psb"])
                        R.op("dve", lambda e, o_=cols[:, 2:3], i_=psb[:, 0:nk]: e.reduce_sum(out=o_, in_=i_, axis=AX.X),
                             reads=["psb"], writes=["c2"])
                        R.op("act", f_act(cols[:, 3:4], snk[:, sc:sc + 1], AF.Exp, bias=cols[:, 1:2], scale=1.0),
                             reads=["c1", "snk"], writes=["c3"])
                        R.op("dve", f_tt(cols[:, 4:5], cols[:, 2:3], cols[:, 3:4], ALU.add), reads=["c2", "c3"], writes=["c4"])
                        R.op("dve", f_recip(cols[:, 5:6], cols[:, 4:5]), reads=["c4"], writes=["c5"])
                        R.op("dve", f_ts(pn[:, 0:nk], psb[:, 0:nk], cols[:, 5:6], None, ALU.mult),
                             reads=["psb", "c5"], writes=["pn"])
                        if alev < 3:
                            continue
                        nj = nk // 128
                        R.multi("pe", [f_tr(ptb[:, i * 128:(i + 1) * 128], pn[:, i * 128:(i + 1) * 128], ident[:])
                                       for i in range(nj)], reads=["pn", "ident"], writes=["ptb"])
                        R.op("act", f_copy_act(pT[:, 0:nk], ptb[:, 0:nk]), reads=["ptb"], writes=["pT"])
                        if alev < 4:
                            continue
                        kb0 = k0 // 128
                        R.multi("pe", [f_mm(pb[6][:, 384:512], Vtm[:, kb0 + i, hk * 128:(hk + 1) * 128],
                                            pT[:, i * 128:(i + 1) * 128], i == 0, i == nj - 1) for i in range(nj)],
                                reads=["Vtm", "pT"], writes=["b6"])
                        R.op("act", f_copy_act(ost[:, hs, q0:q0 + 128], pb[6][:, 384:512]), reads=["b6"], writes=[("ost", hs)])
                    R.dma("sp", attn_h[h * 128:(h + 1) * 128, 0:Tq], ost[:, hs, 0:Tq], reads=[("ost", hs)], writes=[("attn_h", h)])
            R.barrier()
            ck(3)
            for t_ in (ubuf, pA, pB):
                R.op("dve", f_memset(t_[:, 0:16], 0.0), writes=["poolbuf"])
            for c in range(16):
                gi = c // 4
                w = 2 << gi
                su = wload(l, "in", 28 + c, 4096)
                s = unit(su, NCH, rhs_x, Tk, XH_KEYS)
                for j, (t0, tn) in enumerate(tiles_of(Tk)):
                    R.op("dve", f_tt(ubuf[:, 16 + t0:16 + t0 + tn], pb[3 * s + j][:, 0:tn], rstd[:, t0:t0 + tn], ALU.mult),
                         reads=[("ps", 3 * s + j), "rstd", "poolbuf"], writes=["ubuf"])
                src = ubuf
                srck = "ubuf"
                for i, sh in enumerate([1, 2, 4, 8][:gi + 1]):
                    dst, dk = (pA, "pA") if i % 2 == 0 else (pB, "pB")
                    R.op("dve", f_tt(dst[:, 16:16 + Tk], src[:, 16:16 + Tk], src[:, 16 - sh:16 + Tk - sh], ALU.add),
                         reads=[srck, "poolbuf"], writes=[dk])
                    src, srck = dst, dk
                e0 = 16 + w // 2 - 1
                R.op("dve", f_ts(pr[:, 0:Tq], src[:, e0:e0 + Tq], poolc[:, 0:1], None, ALU.mult),
                     reads=[srck, "poolc"], writes=["pr"])
                R.op("dve", f_stt(pr[:, 0:Tq], src[:, e0 + 1:e0 + 1 + Tq], poolc[:, 1:2], pr[:, 0:Tq], ALU.mult, ALU.add),
                     reads=[srck, "pr", "poolc"], writes=["pr"])
                R.op("dve", f_ts(pr[:, 0:Tq], pr[:, 0:Tq], 1.0 / w, None, ALU.mult), reads=["pr"], writes=["pr"])
                R.op("dve", f_tt(pr[:, 0:16], pr[:, 0:16], poolc[:, 2 + gi * 16:2 + (gi + 1) * 16], ALU.mult),
                     reads=["pr", "poolc"], writes=["pr"])
                R.op("dve", f_tt(pooled[:, c % 4, 0:Tq], pr[:, 0:Tq], ubuf[:, 16:16 + Tq], ALU.subtract),
                     reads=["pr", "ubuf"], writes=[("pooled", c % 4)])
                if c % 4 == 3:
                    for oc in range(4):
                        spw = wload(l, "pw", gi * 4 + oc, 512)
                        s = unit(spw, 4, lambda kc, t0, tn: pooled[:, kc, t0:t0 + tn], Tq,
                                 [("pooled", i) for i in range(4)])
                        m = (gi * 4 + oc)
                        for j, (t0, tn) in enumerate(tiles_of(Tq)):
                            R.op("act", f_act(mst[:, m % 2, t0:t0 + tn], pb[3 * s + j][:, 0:tn], AF.Copy,
                                              scale=psc[:, l * 16 + m:l * 16 + m + 1]),
                                 reads=[("ps", 3 * s + j), "psc"], writes=[("mst", m % 2)])
                        R.dma("sp", mixed_h[m * 128:(m + 1) * 128, 0:Tq], mst[:, m % 2, 0:Tq],
                              reads=[("mst", m % 2)], writes=[("mixed_h", m)])
            ck(4)
            for which in range(2):
                for o in range(NCH):
                    k = which * NCH + o
                    sg_ = wload(l, "in", 44 + k, 4096)
                    s = unit(sg_, NCH, rhs_x, Tq, XH_KEYS)
                    for j, (t0, tn) in enumerate(tiles_of(Tq)):
                        ti = j % 2
                        R.op("dve", f_tt(TMP[:, ti, 0:tn], pb[3 * s + j][:, 0:tn], rstd[:, t0:t0 + tn], ALU.mult),
                             reads=[("ps", 3 * s + j), "rstd"], writes=[("tmp", ti)])
                        R.op("act", f_act(sgst[:, k % 2, t0:t0 + tn], TMP[:, ti, 0:tn], AF.Sigmoid),
                             reads=[("tmp", ti)], writes=[("sgst", k % 2)])
                    R.dma("sp", sg_h[k * 128:(k + 1) * 128, 0:Tq], sgst[:, k % 2, 0:Tq],
                          reads=[("sgst", k % 2)], writes=[("sg_h", k)])
            R.barrier()
            ck(5)
            R.dma("sp", YH[:, 0:16, 0:Tq], attn_h[:, 0:Tq].rearrange("(c p) t -> p c t", p=128), writes=["am0"])
            R.dma("sp", YH[:, 16:32, 0:Tq], mixed_h[:, 0:Tq].rearrange("(c p) t -> p c t", p=128), writes=["am1"])
            m1 = STG[:, 2, :]
            for o in range(NCH):
                gs = o % 2
                R.dma("sp", GST[:, gs, 0, 0:Tq], sg_h[o * 128:(o + 1) * 128, 0:Tq], writes=[("gst", gs, 0)])
                R.dma("sp", GST[:, gs, 1, 0:Tq], sg_h[(NCH + o) * 128:(NCH + o + 1) * 128, 0:Tq], writes=[("gst", gs, 1)])
                sa = wload(l, "br", 2 * o, 2048)
                s = unit(sa, 16, lambda kc, t0, tn: YH[:, kc, t0:t0 + tn], Tq, ["am0"])
                for j, (t0, tn) in enumerate(tiles_of(Tq)):
                    R.op("dve", f_tt(m1[:, t0:t0 + tn], pb[3 * s + j][:, 0:tn], GST[:, gs, 0, t0:t0 + tn], ALU.mult),
                         reads=[("ps", 3 * s + j), ("gst", gs, 0)], writes=["m1"])
                sb_ = wload(l, "br", 2 * o + 1, 2048)
                s = unit(sb_, 16, lambda kc, t0, tn: YH[:, 16 + kc, t0:t0 + tn], Tq, ["am1"])
                for j, (t0, tn) in enumerate(tiles_of(Tq)):
                    ti = j % 2
                    R.op("dve", f_tt(TMP[:, ti, 0:tn], pb[3 * s + j][:, 0:tn], GST[:, gs, 1, t0:t0 + tn], ALU.mult),
                         reads=[("ps", 3 * s + j), ("gst", gs, 1)], writes=[("tmp", ti)])
                    R.op("dve", f_tt(XA[:, o, t0:t0 + tn], TMP[:, ti, 0:tn], m1[:, t0:t0 + tn], ALU.add),
                         reads=[("tmp", ti), "m1"], writes=[("mg", o)])
            R.barrier()
            ck(6)
            MG_KEYS = [("mg", o) for o in range(NCH)]

            def resid_unit(o, slot_w, KC, rhs, keys, stg_slots):
                ss = stg_slots[o % len(stg_slots)]
                R.dma("sp", STG[:, ss, 0:Tq], xres[o * 128:(o + 1) * 128, 0:Tq], reads=[("xres", o)], writes=[("stg", ss)])
                s = unit(slot_w, KC, rhs, Tq, keys)
                for j, (t0, tn) in enumerate(tiles_of(Tq)):
                    R.op("dve", f_tt(STG[:, ss, t0:t0 + tn], pb[3 * s + j][:, 0:tn], STG[:, ss, t0:t0 + tn], ALU.add),
                         reads=[("ps", 3 * s + j), ("stg", ss)], writes=[("stg", ss)])
                R.dma("sp", xres[o * 128:(o + 1) * 128, 0:Tq], STG[:, ss, 0:Tq], reads=[("stg", ss)], writes=[("xres", o)])

            for o in range(NCH):
                so = wload(l, "out", o, 4096)
                resid_unit(o, so, NCH, lambda kc, t0, tn: XA[:, kc, t0:t0 + tn], MG_KEYS, (0, 1, 2, 3))
            R.barrier()
            ck(7)
            prep(xres, Tq, YH, (2 * l + 1) * NCH, accX, sqX, False, rstdX)
            rstd = rstdX
            R.barrier()
            ck(8)
            def rhs_h2(kc, t0, tn):
                return YH[:, kc, t0:t0 + tn]

            for fgi, (f0, f1) in enumerate(FGROUPS):
                G = f1 - f0
                for fi in range(G):
                    f = f0 + fi
                    sgs = fi % 2
                    swg = wload(l, "gu", 2 * f, 4096)
                    s = unit(swg, NCH, rhs_h2, Tq, XH_KEYS)
                    for j, (t0, tn) in enumerate(tiles_of(Tq)):
                        ti = j % 2
                        R.op("dve", f_tt(TMP[:, ti, 0:tn], pb[3 * s + j][:, 0:tn], rstd[:, t0:t0 + tn], ALU.mult),
                             reads=[("ps", 3 * s + j), "rstd"], writes=[("tmp", ti)])
                        R.op("act", f_act(STG[:, sgs, t0:t0 + tn], TMP[:, ti, 0:tn], AF.Silu),
                             reads=[("tmp", ti)], writes=[("stg", sgs)])
                    swu = wload(l, "gu", 2 * f + 1, 4096)
                    s = unit(swu, NCH, rhs_h2, Tq, XH_KEYS)
                    for j, (t0, tn) in enumerate(tiles_of(Tq)):
                        ti = 2 + j % 2
                        R.op("dve", f_tt(TMP[:, ti, 0:tn], pb[3 * s + j][:, 0:tn], rstd[:, t0:t0 + tn], ALU.mult),
                             reads=[("ps", 3 * s + j), "rstd"], writes=[("tmp", ti)])
                        R.op("dve", f_tt(actb[:, fi, t0:t0 + tn], TMP[:, ti, 0:tn], STG[:, sgs, t0:t0 + tn], ALU.mult),
                             reads=[("tmp", ti), ("stg", sgs)], writes=[("act", fi)])
                for o in range(NCH):
                    if fgi < 5:
                        sd = wload(l, "dn", fgi * NCH + o, 2048)
                    else:
                        sd = wload(l, "dl", o, 768)
                    resid_unit(o, sd, G, lambda kc, t0, tn: actb[:, kc, t0:t0 + tn],
                               [("act", i) for i in range(G)], (2, 3))
            R.barrier()
            ck(9)

        def final(vs):
            prep(xres, OWN, None, 0, accX, sqX, False, rstdX)
            rstd = rstdX
            for o in range(NCH):
                slot = o % 4
                R.dma("sp", STG[:, slot, 0:OWN], xres[o * 128:(o + 1) * 128, 0:OWN], reads=[("xres", o)], writes=[("stg", slot)])
                R.op("dve", f_tt(STG[:, slot, 0:OWN], STG[:, slot, 0:OWN], rstd[:, 0:OWN], ALU.mult),
                     reads=[("stg", slot), "rstd"], writes=[("stg", slot)])
                R.op("act", f_act(STG[:, slot, 0:OWN], STG[:, slot, 0:OWN], AF.Copy, scale=gv[:, 4 * NCH + o:4 * NCH + o + 1]),
                     reads=[("stg", slot), "gv"], writes=[("stg", slot)])
                R.dma("sp", yT[vs * D + o * 128:vs * D + (o + 1) * 128, :], STG[:, slot, 0:OWN], reads=[("stg", slot)], writes=[("yT", vs, o)])
            R.barrier()

        for vs in range(nvs):
            try:
                layers(vs)
            except _Stop:
                R.barrier()
            final(vs)
        R.wait_only("sp", [(s, s.n, "dma") for s in dsem["sp"] if s.n])
        R.wait_only("pool", [(s, s.n, "dma") for s in dsem["pool"] if s.n])

        with nc.Block() as block:
            @block.sync
            def _(e):
                R.replay("sp", e)

            @block.gpsimd
            def _(e):
                R.replay("pool", e)

            @block.tensor
            def _(e):
                R.replay("pe", e)

            @block.vector
            def _(e):
                R.replay("dve", e)

            @block.scalar
            def _(e):
                R.replay("act", e)
    return nc


def f_copy_act(out, in_):
    return lambda e: e.activation(out=out, in_=in_, func=AF.Copy)


def _part(base, which):
    if which == 1:
        return [base + d for d in range(0, 16)] + [base + d for d in range(32, 80)]
    return [base + d for d in range(16, 32)] + [base + d for d in range(80, 128)]


def _in_cols():
    cols = []
    for hk in range(4):
        b = 2048 + hk * 128
        cols += _part(b, 1) + _part(b, 1)
        cols += _part(b, 2) + _part(b, 2)
    for hk in range(4):
        cols += list(range(2560 + hk * 128, 2560 + (hk + 1) * 128))
    for p in range(8):
        a, b = (2 * p) * 128, (2 * p + 1) * 128
        cols += _part(a, 1) + _part(b, 1)
        cols += _part(a, 2) + _part(b, 2)
    cols += list(range(3072, 13312))
    return np.asarray(cols, dtype=np.int64)


def _opanels(W, cols=None):
    if cols is not None:
        W = W[:, cols]
    K, N = W.shape
    KC, NP = K // 128, N // 128
    return np.ascontiguousarray(W.reshape(KC, 128, NP, 128).transpose(2, 1, 0, 3)).reshape(NP * 128, K)


def _layer_groups(l, w_in, pool_w, wba, wbp, w_out, w_gu, w_dn):
    g = {}
    g["in"] = _opanels(w_in[l], _in_cols())
    pw = pool_w[l]
    g["pw"] = np.concatenate([_opanels(pw[gi]) for gi in range(4)], axis=0)
    a = _opanels(wba[l]).reshape(32, 128, 2048)
    b = _opanels(wbp[l]).reshape(32, 128, 2048)
    g["br"] = np.ascontiguousarray(np.stack([a, b], axis=1)).reshape(64 * 128, 2048)
    g["out"] = _opanels(w_out[l])
    gu = w_gu[l]
    gcols = np.stack([np.arange(NF * 128).reshape(NF, 128), DFF + np.arange(NF * 128).reshape(NF, 128)], axis=1).reshape(-1)
    g["gu"] = _opanels(gu, gcols)
    wd = w_dn[l].reshape(NF, 128, 32, 128)
    dn = []
    for (f0, f1) in FGROUPS[:5]:
        dn.append(np.ascontiguousarray(wd[f0:f1].transpose(2, 1, 0, 3)).reshape(32 * 128, (f1 - f0) * 128))
    g["dn"] = np.concatenate(dn, axis=0)
    f0, f1 = FGROUPS[5]
    g["dl"] = np.ascontiguousarray(wd[f0:f1].transpose(2, 1, 0, 3)).reshape(32 * 128, (f1 - f0) * 128)
    return g


def _tables(rev):
    pos = np.arange(TX, dtype=np.float32)
    if rev:
        pos = (SEQ - 1) - pos
    inv_freq = (1.0 / np.power(np.float32(500000.0), np.arange(0, 32, 2, dtype=np.float32) / np.float32(32))).astype(np.float32)
    ang = (pos[None, :] * inv_freq[:, None]).astype(np.float32)
    c, s = np.cos(ang).astype(np.float32), np.sin(ang).astype(np.float32)
    ct = np.ones((128, TX), np.float32)
    st = np.zeros((128, TX), np.float32)
    for r in (0, 64):
        ct[r:r + 16] = c
        st[r:r + 16] = s
    pc = np.zeros((128, 66), np.float32)
    pc[:, 0] = 0.0 if rev else 1.0
    pc[:, 1] = 1.0 if rev else 0.0
    for gi, w in enumerate((2, 4, 8, 16)):
        t = np.arange(16)
        if rev:
            cnt = w - np.maximum(0, (w // 2 - 1) - t)
        else:
            cnt = w - np.maximum(0, w // 2 - t)
        pc[:, 2 + gi * 16:2 + (gi + 1) * 16] = (np.float32(w) / cnt.astype(np.float32))[None, :]
    return ct, st, pc


def _masks():
    m = np.full((128, 640), NEG, np.float32)
    qi = np.arange(128)[:, None]
    kj = np.arange(256)[None, :]
    m[:, 0:256][np.abs(kj - qi) <= 128] = 0.0
    kj = np.arange(384)[None, :]
    m[:, 256:640][np.abs(kj - 128 - qi) <= 128] = 0.0
    return m


def _core_tokens(s):
    if s == 0:
        return np.arange(0, TX)
    return (SEQ - 1) - np.arange(0, TX)


def make_in_maps(inputs, NL=DEPTH):
    x = np.asarray(inputs["x"], np.float32)
    f = lambda k: np.asarray(inputs[k], np.float32)
    w_in, pool_w, wba, wbp = f("w_in"), f("pool_w"), f("w_branch_attn"), f("w_branch_pool")
    w_out, w_gu, w_dn = f("w_out"), f("w_gate_up"), f("w_down")
    gl = [f("norm1_g")[0], f("norm2_g")[0], f("norm1_g")[1], f("norm2_g")[1], f("final_norm_g")]
    gvec = np.concatenate([g.reshape(NCH, 128).T for g in gl], axis=1).astype(np.float32)
    pscale = np.concatenate([f("pool_scale")[l].reshape(16, 128).T for l in range(DEPTH)], axis=1).astype(np.float32)
    sinkb = np.broadcast_to(f("attn_sink").reshape(1, DEPTH * 16), (128, DEPTH * 16)).astype(np.float32).copy()
    masks = _masks()
    ident = np.eye(128, dtype=np.float32).astype(ml_dtypes.bfloat16)
    tabs = [_tables(0), _tables(1)]
    maps = []
    for c in range(NCORES):
        xs = []
        ct = np.zeros((256, TX), np.float32)
        st = np.zeros((256, TX), np.float32)
        pc = np.zeros((256, 66), np.float32)
        for vs in range(NVS):
            gs = c * NVS + vs
            b, s = gs // 2, gs % 2
            xs.append(x[b, _core_tokens(s), :].T)
            slot = vs % 2
            ct[slot * 128:(slot + 1) * 128] = tabs[s][0]
            st[slot * 128:(slot + 1) * 128] = tabs[s][1]
            pc[slot * 128:(slot + 1) * 128] = tabs[s][2]
        m = {
            "xT": np.ascontiguousarray(np.concatenate(xs, axis=0)),
            "gvec": gvec, "pscale": pscale, "sinkb": sinkb, "ctab": ct, "stab": st,
            "masks": masks, "ident": ident, "poolc": pc,
        }
        maps.append(m)
    for l in range(NL):
        groups = _layer_groups(l, w_in, pool_w, wba, wbp, w_out, w_gu, w_dn)
        for (g, npan, W) in WGROUPS:
            arr = groups[g]
            assert arr.shape == (npan * 128, W), (g, arr.shape)
            for c in range(NCORES):
                maps[c][f"w{l}_{g}"] = arr
        del groups
    return maps


def assemble(results):
    out = np.empty((4, SEQ, D), np.float32)
    for c in range(NCORES):
        yc = np.asarray(results[c]["yT"]).reshape(NVS, D, OWN)
        for vs in range(NVS):
            gs = c * NVS + vs
            b, s = gs // 2, gs % 2
            tok = _core_tokens(s)[:OWN]
            out[b, tok, :] = yc[vs].T
    return out


_NC_CACHE = {}


def kernel(**inputs):
    if "nc" not in _NC_CACHE:
        _NC_CACHE["nc"] = build()
    nc = _NC_CACHE["nc"]
    in_maps = make_in_maps(inputs)
    res = run_bass_kernel_spmd(nc, in_maps, core_ids=list(range(NCORES)))
    return assemble(res.results)
```
